# Optimizing a Trainium2 kernel written in Bass

```python
import math
import jax, jax.numpy as jnp
from jax import lax
import numpy as np

D_MODEL = 2048
BATCH = 2
SEQ = 4096
DEPTH = 2

GRID_W = 64
CTX_LEN = 256
EPS = 1e-6

POOL_GROUPS = 4
POOL_WINDOWS = (2, 4, 8, 16)
POOL_W = D_MODEL // 2
POOL_GC = POOL_W // POOL_GROUPS
POOL_OUT = D_MODEL // POOL_GROUPS

FOUR_GROUPS = 4
FOUR_W = D_MODEL // 2
FOUR_GC = FOUR_W // FOUR_GROUPS

D_SSD = D_MODEL
HEAD_DIM = 64
SSD_HEADS = D_SSD // HEAD_DIM
SSD_GROUPS = 4
HEADS_PER_GROUP = SSD_HEADS // SSD_GROUPS
D_STATE = 128
CONV_W = 5
CHUNK = 128

D_FF = ((8 * D_MODEL // 3 + 255) // 256) * 256

XBC = D_SSD + 2 * SSD_GROUPS * D_STATE
SSD_IN = XBC + 2 * SSD_HEADS
Z_OFF = SSD_IN
POOL_OFF = Z_OFF + D_SSD
FOUR_OFF = POOL_OFF + POOL_W
GATE_OFF = FOUR_OFF + FOUR_W
N_IN = GATE_OFF + 3 * D_MODEL

kernel_name = "hybrid_pool_fourier_ssd_prefix_dit"


def _rmsnorm(x, w):
    xf = x.astype(jnp.float32)
    y = xf * lax.rsqrt(jnp.mean(xf * xf, axis=-1, keepdims=True) + EPS)
    return y.astype(x.dtype) * w


def _swiglu(h, w_gate, w_up, w_down):
    return (jax.nn.silu(h @ w_gate) * (h @ w_up)) @ w_down


def _box_mean(v, w, axis):
    n = v.shape[axis]
    idx = jnp.arange(n)
    lo = jnp.clip(idx - w // 2, 0, n)
    hi = jnp.clip(idx + (w - w // 2), 0, n)
    pad = [(0, 0)] * v.ndim
    pad[axis] = (1, 0)
    cs = jnp.pad(jnp.cumsum(v, axis=axis), pad)
    s = jnp.take(cs, hi, axis=axis) - jnp.take(cs, lo, axis=axis)
    shape = [1] * v.ndim
    shape[axis] = n
    return s / (hi - lo).astype(jnp.float32).reshape(shape)


def _pool_branch(u, rows, w_pool, pool_scale):
    b, L, _ = u.shape
    ug = u.astype(jnp.float32).reshape(b, L, POOL_GROUPS, POOL_GC)
    outs = []
    for g, w in enumerate(POOL_WINDOWS):
        v = ug[:, :, g]
        if rows is None:
            m = _box_mean(v, w, 1)
        else:
            vg = v.reshape(b, rows, GRID_W, POOL_GC)
            m = _box_mean(_box_mean(vg, w, 1), w, 2).reshape(b, L, POOL_GC)
        outs.append(m - v)
    d = jnp.stack(outs, axis=2).astype(u.dtype)
    y = jnp.einsum('blgc,gce->blge', d, w_pool).reshape(b, L, D_MODEL)
    return y * pool_scale


def _fourier_branch(u, w_fourier):
    b, L, _ = u.shape
    ug = u.astype(jnp.float32).reshape(b, L, FOUR_GROUPS, FOUR_GC)
    f = jnp.fft.fftn(ug, axes=(1, 3), norm="ortho").real
    return f.reshape(b, L, FOUR_W).astype(u.dtype) @ w_fourier


def _dwconv(u, w, bias):
    y = lax.conv_general_dilated(
        u, w[:, None, :], window_strides=(1,),
        padding=[(CONV_W // 2, CONV_W // 2)],
        dimension_numbers=('NWC', 'WIO', 'NWC'),
        feature_group_count=u.shape[-1])
    return y + bias


def _segsum(a):
    T = a.shape[-1]
    ar = jnp.broadcast_to(a[..., :, None], a.shape + (T,))
    ar = jnp.where(jnp.tril(jnp.ones((T, T), bool), -1), ar, 0.0)
    cs = jnp.cumsum(ar, axis=-2)
    return jnp.where(jnp.tril(jnp.ones((T, T), bool)), cs, -jnp.inf)


def _ssd_scan(xs, dt, a, bm, cm, h0, with_output):
    b, L, G, J, P = xs.shape
    N = bm.shape[-1]
    nc = L // CHUNK
    xd = (xs.astype(jnp.float32) * dt[..., None]).reshape(b, nc, CHUNK, G, J, P)
    bc = bm.astype(jnp.float32).reshape(b, nc, CHUNK, G, N)
    cc = cm.astype(jnp.float32).reshape(b, nc, CHUNK, G, N)
    da = (dt * a).reshape(b, nc, CHUNK, G, J).transpose(0, 3, 4, 1, 2)
    acs = jnp.cumsum(da, axis=-1)
    decay_to_end = jnp.exp(acs[..., -1:] - acs).transpose(0, 3, 4, 1, 2)
    states = jnp.einsum('bclgn,bclgjp->bcgjpn', bc, xd * decay_to_end[..., None])
    states = jnp.concatenate([h0[:, None], states], axis=1)
    chunk_decay = jnp.exp(_segsum(jnp.pad(acs[..., -1], ((0, 0), (0, 0), (0, 0), (1, 0)))))
    states = jnp.einsum('bgjzc,bcgjpn->bzgjpn', chunk_decay, states)
    final = states[:, -1]
    if not with_output:
        return None, final
    lm = jnp.exp(_segsum(da))
    cb = jnp.einsum('bclgn,bcsgn->bgcls', cc, bc)
    y_diag = jnp.einsum('bgjcls,bcsgjp->bclgjp', cb[:, :, None] * lm, xd)
    y_off = jnp.einsum('bclgn,bcgjpn->bclgjp', cc, states[:, :-1]) * \
        jnp.exp(acs).transpose(0, 3, 4, 1, 2)[..., None]
    y = (y_diag + y_off).reshape(b, L, G, J, P)
    return y.astype(xs.dtype), final


def _ssd_core(p_ssd, conv_w, conv_b, a_log, dt_bias, d_skip, h0f, h0b, with_output):
    b, L, _ = p_ssd.shape
    G, J = SSD_GROUPS, HEADS_PER_GROUP
    xbc = jax.nn.silu(_dwconv(p_ssd[..., :XBC], conv_w, conv_b))
    xs = xbc[..., :D_SSD].reshape(b, L, G, J, HEAD_DIM)
    bm = xbc[..., D_SSD:D_SSD + G * D_STATE].reshape(b, L, G, D_STATE)
    cm = xbc[..., D_SSD + G * D_STATE:].reshape(b, L, G, D_STATE)
    dt = jax.nn.softplus(p_ssd[..., XBC:].astype(jnp.float32).reshape(b, L, 2, G, J)
                         + dt_bias.astype(jnp.float32).reshape(2, G, J))
    a = -jnp.exp(a_log.astype(jnp.float32)).reshape(2, G, J)
    rev = lambda t: jnp.flip(t, axis=1)
    yf, hf = _ssd_scan(xs, dt[:, :, 0], a[0], bm, cm, h0f, with_output)
    yb, hb = _ssd_scan(rev(xs), rev(dt[:, :, 1]), a[1], rev(bm), rev(cm), h0b, with_output)
    if not with_output:
        return None, hf, hb
    y = yf + rev(yb) + d_skip.reshape(G, J)[..., None] * xs
    return y, hf, hb


def _mixer(p, rows, h0f, h0b, conv_w, conv_b, a_log, dt_bias, d_skip, ssd_norm_w,
           w_ssd_out, w_pool, pool_scale, w_fourier, w_out):
    b, L, _ = p.shape
    G = SSD_GROUPS
    gw = HEADS_PER_GROUP * HEAD_DIM
    y, hf, hb = _ssd_core(p[..., :SSD_IN], conv_w, conv_b, a_log, dt_bias, d_skip, h0f, h0b, True)
    z = p[..., Z_OFF:POOL_OFF].reshape(b, L, G, gw)
    y = _rmsnorm(y.reshape(b, L, G, gw) * jax.nn.silu(z), ssd_norm_w.reshape(G, gw))
    y_ssd = y.reshape(b, L, D_SSD) @ w_ssd_out
    y_pool = _pool_branch(p[..., POOL_OFF:FOUR_OFF], rows, w_pool, pool_scale)
    y_four = _fourier_branch(p[..., FOUR_OFF:GATE_OFF], w_fourier)
    g = jax.nn.sigmoid(p[..., GATE_OFF:].astype(jnp.float32)).astype(p.dtype).reshape(b, L, 3, D_MODEL)
    merged = g[:, :, 0] * y_pool + g[:, :, 1] * y_four + g[:, :, 2] * y_ssd
    return merged @ w_out, hf, hb


def setup_inputs(seed: int = 0) -> dict:
    key = jax.random.key(seed)
    ks = jax.random.split(key, 26)
    f32 = jnp.float32
    nrm = lambda k, shape, s: jax.random.normal(k, shape, f32) * s
    dt0 = jnp.exp(jax.random.uniform(ks[12], (DEPTH, 2, SSD_HEADS), f32, math.log(1e-3), math.log(1e-1)))
    return {
        "x": nrm(ks[0], (BATCH, SEQ, D_MODEL), 1.0),
        "c": nrm(ks[1], (BATCH, D_MODEL), 1.0),
        "ctx": nrm(ks[2], (BATCH, CTX_LEN, D_MODEL), 1.0),
        "c_ctx": nrm(ks[3], (D_MODEL,), 1.0),
        "w_ada": nrm(ks[4], (DEPTH, D_MODEL, 6 * D_MODEL), 0.5 * D_MODEL ** -0.5),
        "b_ada": nrm(ks[5], (DEPTH, 6 * D_MODEL), 0.01),
        "norm_mix_w": 1.0 + nrm(ks[6], (DEPTH, D_MODEL), 0.05),
        "norm_ffn_w": 1.0 + nrm(ks[7], (DEPTH, D_MODEL), 0.05),
        "w_in": nrm(ks[8], (DEPTH, D_MODEL, N_IN), D_MODEL ** -0.5),
        "conv_w": nrm(ks[9], (DEPTH, CONV_W, XBC), CONV_W ** -0.5),
        "conv_b": nrm(ks[10], (DEPTH, XBC), 0.01),
        "a_log": jnp.log(jax.random.uniform(ks[11], (DEPTH, 2, SSD_HEADS), f32, 1.0, 16.0)),
        "dt_bias": dt0 + jnp.log(-jnp.expm1(-dt0)),
        "d_skip": 1.0 + nrm(ks[13], (DEPTH, SSD_HEADS), 0.1),
        "ssd_norm_w": 1.0 + nrm(ks[14], (DEPTH, D_SSD), 0.05),
        "w_ssd_out": nrm(ks[15], (DEPTH, D_SSD, D_MODEL), D_SSD ** -0.5),
        "w_pool": nrm(ks[16], (DEPTH, POOL_GROUPS, POOL_GC, POOL_OUT), POOL_GC ** -0.5),
        "pool_scale": 1.0 + nrm(ks[17], (DEPTH, D_MODEL), 0.05),
        "w_fourier": nrm(ks[18], (DEPTH, FOUR_W, D_MODEL), FOUR_W ** -0.5),
        "w_out": nrm(ks[19], (DEPTH, D_MODEL, D_MODEL), D_MODEL ** -0.5),
        "w_ffn_gate": nrm(ks[20], (DEPTH, D_MODEL, D_FF), D_MODEL ** -0.5),
        "w_ffn_up": nrm(ks[21], (DEPTH, D_MODEL, D_FF), D_MODEL ** -0.5),
        "w_ffn_down": nrm(ks[22], (DEPTH, D_FF, D_MODEL), D_FF ** -0.5),
        "final_norm_w": 1.0 + nrm(ks[23], (D_MODEL,), 0.05),
    }


def reference(x, c, ctx, c_ctx, w_ada, b_ada, norm_mix_w, norm_ffn_w, w_in, conv_w, conv_b,
              a_log, dt_bias, d_skip, ssd_norm_w, w_ssd_out, w_pool, pool_scale, w_fourier,
              w_out, w_ffn_gate, w_ffn_up, w_ffn_down, final_norm_w):
    rows = x.shape[1] // GRID_W
    h0 = jnp.zeros((ctx.shape[0], SSD_GROUPS, HEADS_PER_GROUP, HEAD_DIM, D_STATE), jnp.float32)
    for l in range(DEPTH):
        last = l == DEPTH - 1
        sh_m, sc_m, g_m, sh_f, sc_f, g_f = [m[:, None, :] for m in
            jnp.split(jax.nn.silu(c) @ w_ada[l] + b_ada[l], 6, axis=-1)]
        csh_m, csc_m, cg_m, csh_f, csc_f, cg_f = jnp.split(
            jax.nn.silu(c_ctx) @ w_ada[l] + b_ada[l], 6, axis=-1)

        hc = _rmsnorm(ctx, norm_mix_w[l]) * (1 + csc_m) + csh_m
        if last:
            _, hf, hb = _ssd_core(hc @ w_in[l][:, :SSD_IN], conv_w[l], conv_b[l], a_log[l],
                                  dt_bias[l], d_skip[l], h0, h0, False)
        else:
            yc, hf, hb = _mixer(hc @ w_in[l], None, h0, h0, conv_w[l], conv_b[l], a_log[l],
                                dt_bias[l], d_skip[l], ssd_norm_w[l], w_ssd_out[l], w_pool[l],
                                pool_scale[l], w_fourier[l], w_out[l])
            ctx = ctx + cg_m * yc
            ctx = ctx + cg_f * _swiglu(_rmsnorm(ctx, norm_ffn_w[l]) * (1 + csc_f) + csh_f,
                                       w_ffn_gate[l], w_ffn_up[l], w_ffn_down[l])

        hl = _rmsnorm(x, norm_mix_w[l]) * (1 + sc_m) + sh_m
        yl, _, _ = _mixer(hl @ w_in[l], rows, hf, hb, conv_w[l], conv_b[l], a_log[l],
                          dt_bias[l], d_skip[l], ssd_norm_w[l], w_ssd_out[l], w_pool[l],
                          pool_scale[l], w_fourier[l], w_out[l])
        x = x + g_m * yl
        x = x + g_f * _swiglu(_rmsnorm(x, norm_ffn_w[l]) * (1 + sc_f) + sh_f,
                              w_ffn_gate[l], w_ffn_up[l], w_ffn_down[l])
    return _rmsnorm(x, final_norm_w)
```

```python
import contextlib
import numpy as np
import ml_dtypes
import concourse.bass as bass
import concourse.mybir as mybir
from concourse.bass_utils import run_bass_kernel_spmd

F32 = mybir.dt.float32
BF16 = mybir.dt.bfloat16
AF = mybir.ActivationFunctionType
ALU = mybir.AluOpType
NPBF = ml_dtypes.bfloat16

D = 2048
DC = 16
NIN = 13376
DFF = 5632
EPS = 1e-6
GCOLS = 1808


class _Buf:
    __slots__ = ("name", "w", "r")

    def __init__(self, name):
        self.name = name
        self.w = None
        self.r = []


class _Op:
    __slots__ = ("eng", "fn", "deps", "need_inc", "cnt", "is_dma", "sem", "val", "inc")

    def __init__(self, eng, fn, is_dma):
        self.eng = eng
        self.fn = fn
        self.deps = []
        self.need_inc = False
        self.cnt = None
        self.is_dma = is_dma
        self.sem = None
        self.val = None
        self.inc = 16


class Prog:
    ENGS = ("tensor", "vector", "scalar", "gpsimd", "sync")

    def __init__(self, nc, stack):
        self.nc = nc
        self.stack = stack
        self.ops = {e: [] for e in self.ENGS}
        self.bufs = {}
        self.dma_keys = {}
        self.esem = {e: stack.enter_context(nc.semaphore("s_" + e)) for e in self.ENGS}
        self.dsem = {}
        self.ecnt = {e: 0 for e in self.ENGS}
        self.known = {e: {} for e in self.ENGS}

    def _b(self, name):
        b = self.bufs.get(name)
        if b is None:
            b = self.bufs[name] = _Buf(name)
        return b

    def _track(self, op, reads, writes):
        deps = []
        rb = [self._b(r) for r in reads]
        wb = [self._b(w) for w in writes]
        for b in rb:
            if b.w is not None:
                deps.append(b.w)
        for b in wb:
            if b.w is not None:
                deps.append(b.w)
            deps.extend(b.r)
        for b in rb:
            b.r.append(op)
        for b in wb:
            b.w = op
            b.r = []
        seen = set()
        for d in deps:
            if d is op or id(d) in seen:
                continue
            seen.add(id(d))
            if (not d.is_dma) and (not op.is_dma) and d.eng == op.eng and d.eng == "tensor":
                continue
            op.deps.append(d)
            d.need_inc = True

    def op(self, eng, fn, reads=(), writes=()):
        o = _Op(eng, fn, False)
        self._track(o, reads, writes)
        self.ops[eng].append(o)
        return o

    def dma(self, queue, out, in_, reads=(), writes=(), key=None, **kw):
        if key is None:
            key = writes[0] if writes else reads[0]
        o = _Op(queue, (lambda e, out=out, in_=in_, kw=kw: e.dma_start(out=out, in_=in_, **kw)), True)
        o.sem = key
        ctr = self.dma_keys.setdefault(key, [0])
        ctr[0] += 16
        o.val = ctr[0]
        self._track(o, reads, writes)
        self.ops[queue].append(o)
        return o

    def cc(self, kind, ins, outs, groups, reads=(), writes=(), key=None):
        o = _Op("gpsimd", (lambda e: e.collective_compute(kind, ALU.bypass, replica_groups=groups, ins=ins, outs=outs)), True)
        o.sem = key
        o.inc = 1
        ctr = self.dma_keys.setdefault(key, [0])
        ctr[0] += 1
        o.val = ctr[0]
        self._track(o, reads, writes)
        self.ops["gpsimd"].append(o)
        return o

    def flush(self, barrier=True):
        nc = self.nc
        for e in self.ENGS:
            for o in self.ops[e]:
                if (not o.is_dma) and o.need_inc:
                    self.ecnt[e] += 1
                    o.cnt = self.ecnt[e]
        for k in self.dma_keys:
            if k not in self.dsem:
                self.dsem[k] = self.stack.enter_context(nc.semaphore("d%d" % len(self.dsem)))
        esem, dsem = self.esem, self.dsem
        with nc.Block() as block:
            def run(engname, E):
                known = self.known[engname]
                for o in self.ops[engname]:
                    waits = {}
                    for d in o.deps:
                        if d.is_dma:
                            s, v, kk = dsem[d.sem], d.val, ("d", d.sem)
                        else:
                            s, v, kk = esem[d.eng], d.cnt, ("e", d.eng)
                        if known.get(kk, 0) >= v:
                            continue
                        if kk not in waits or waits[kk][1] < v:
                            waits[kk] = (s, v)
                    for kk, (s, v) in waits.items():
                        E.wait_ge(s, v)
                        known[kk] = v
                    ins = o.fn(E)
                    if o.is_dma:
                        ins.then_inc(dsem[o.sem], o.inc)
                    elif o.need_inc:
                        ins.then_inc(esem[engname], 1)
                if engname == "sync":
                    for k, ctr in self.dma_keys.items():
                        if known.get(("d", k), 0) < ctr[0]:
                            E.wait_ge(dsem[k], ctr[0])
                            known[("d", k)] = ctr[0]

            @block.tensor
            def _(E):
                run("tensor", E)

            @block.vector
            def _(E):
                run("vector", E)

            @block.scalar
            def _(E):
                run("scalar", E)

            @block.gpsimd
            def _(E):
                run("gpsimd", E)

            @block.sync
            def _(E):
                run("sync", E)
        self.ops = {e: [] for e in self.ENGS}
        self.bufs = {}
        if barrier:
            nc.all_engine_barrier()


def _bc(ap, shape, axis):
    return ap.unsqueeze(axis).to_broadcast(shape)


def build_mixer(LAT, CTXL, dbg=False, cc=False):
    LT = CTXL + LAT
    NCH = LT // 128
    NCC = CTXL // 128
    NLB = LAT // 512
    NLC = LAT // 128
    TB = 256
    nc = bass.Bass("TRN2", target_bir_lowering=False)
    din = lambda n, s, d: nc.dram_tensor(n, s, d, kind="ExternalInput").ap()
    dout = lambda n, s, d: nc.dram_tensor(n, s, d, kind="ExternalOutput").ap()
    QC, QL = CTXL // 4, LAT // 4
    QT = QC + QL
    if cc:
        xq = din("xq", [DC, 128, QT], F32)
    else:
        xT = din("xT", [DC, 128, LT], F32)
    wg = din("wg", [DC, 128, GCOLS], F32)
    vecs = din("vecs", [128, 5, DC], F32)
    convp = din("convp", [128, 6, 6], F32)
    tokc = din("tokc", [128, 1056], F32)
    masks = din("masks", [128, 6, 128], F32)
    identb_d = din("identb", [128, 128], BF16)
    ccsc = din("ccsc", [128, 2, 512], BF16)
    if cc:
        dftq = din("dftq", [NLC // 8, 128, 2, LAT], BF16)
        dftq_b = nc.dram_tensor("dftq_b", [NLC // 8, 128, 2, LAT], BF16)
        dftL_t = nc.dram_tensor("dftL_g", [NLC, 128, 2, LAT], BF16)
        dftL = dftL_t.ap()
        xq_b = [nc.dram_tensor("xq_b%d" % k, [4, 128, QT], F32) for k in range(4)]
        xg = [nc.dram_tensor("xg%d" % k, [32, 128, QT], F32) for k in range(4)]
        bsel_d = din("bsel", [128, 2], F32)
    else:
        dftL = din("dftL", [NLC, 128, 2, LAT], BF16)
    dftC = din("dftC", [NCC, 128, 2, CTXL], BF16)
    pband = din("pband", [NLB, 12, 128, 512], BF16)
    pctx = din("pctx", [NCC, 128, CTXL], BF16)
    yn_o = dout("yn", [LT, 512], BF16)
    dT_o = dout("dT", [2, 128, LT], BF16)
    fT_o = dout("fT", [2, 128, LT], BF16)
    s_xbc = nc.dram_tensor("s_xbc", [6, 128, LT], F32).ap()
    s_sz = nc.dram_tensor("s_sz", [LT, 512], F32).ap()
    s_v = nc.dram_tensor("s_v", [NCH, 128, 256], BF16).ap()
    s_ab = nc.dram_tensor("s_ab", [NCH, 128, 512], BF16).ap()
    if dbg:
        dbg_xbc = dout("dbg_xbc", [6, 128, LT], BF16)
        dbg_dt = dout("dbg_dt", [128, NCH, 16], F32)
        dbg_y = dout("dbg_y", [LT, 512], F32)

    h8 = lambda ap: ap.rearrange("p (h q) -> p h q", h=8)
    with contextlib.ExitStack() as glob:
        gsb = lambda n, s, d: glob.enter_context(nc.sbuf_tensor(n, s, d))
        P = Prog(nc, glob)
        V = lambda fn, r, w: P.op("vector", fn, r, w)
        S = lambda fn, r, w: P.op("scalar", fn, r, w)
        G = lambda fn, r, w: P.op("gpsimd", fn, r, w)
        T = lambda fn, r, w: P.op("tensor", fn, r, w)
        pb = [glob.enter_context(nc.psum_tensor("pb%d" % i, [128, 512], F32)) for i in range(8)]

        vec = gsb("vec", [128, 5, DC], F32)
        s1l = gsb("s1l", [128, DC], F32)
        s1c = gsb("s1c", [128, DC], F32)
        cw = gsb("cw", [128, 6, 6], F32)
        tk = gsb("tk", [128, 1056], F32)
        aneg = gsb("aneg", [128, 16], F32)
        mk = gsb("mk", [128, 6, 128], F32)
        identb = gsb("identb_s", [128, 128], BF16)
        onesb = gsb("onesb", [128, 128], BF16)
        ccb = gsb("ccb", [128, 2, 512], BF16)
        epsb = gsb("epsb", [128, 1], F32)
        dt_all = gsb("dt_all", [128, NCH, 16], F32)
        P.dma("sync", vec[:], vecs, writes=["vec"])
        P.dma("sync", cw[:], convp, writes=["cw"])
        P.dma("sync", tk[:], tokc, writes=["tk"])
        P.dma("sync", mk[:], masks, writes=["mk"])
        P.dma("sync", identb[:], identb_d, writes=["identb"])
        P.dma("sync", ccb[:], ccsc, writes=["ccb"])
        V(lambda E: E.memset(onesb[:], 1.0), [], ["onesb"])
        V(lambda E: E.memset(epsb[:], EPS), [], ["epsb"])
        V(lambda E: E.scalar_tensor_tensor(out=s1l[:], in0=vec[:, 1, :], scalar=1.0, in1=vec[:, 0, :], op0=ALU.add, op1=ALU.mult), ["vec"], ["s1l"])
        V(lambda E: E.scalar_tensor_tensor(out=s1c[:], in0=vec[:, 3, :], scalar=1.0, in1=vec[:, 0, :], op0=ALU.add, op1=ALU.mult), ["vec"], ["s1c"])
        S(lambda E: E.activation(out=aneg[:], in_=tk[:, 16:32], func=AF.Exp), ["tk"], ["aneg"])
        V(lambda E: E.tensor_scalar(out=aneg[:], in0=aneg[:], scalar1=-1.0, scalar2=None, op0=ALU.mult), ["aneg"], ["aneg"])

        if cc:
            for k in range(4):
                P.dma("gpsimd", xq_b[k].ap(), xq[4 * k:4 * k + 4], writes=["xq_b%d" % k])
                P.cc("AllGather", [xq_b[k].ap().opt()], [xg[k].ap().opt()], [list(range(8))], reads=["xq_b%d" % k], writes=["xg%d" % k], key="ccx%d" % k)
            P.dma("gpsimd", dftq_b.ap(), dftq, writes=["dftq_b"])
            P.cc("AllGather", [dftq_b.ap().opt()], [dftL_t.ap().opt()], [list(range(8))], reads=["dftq_b"], writes=["dftg"], key="ccd")

        def load_x(xt, xn, tok0):
            if not cc:
                for h2 in range(2):
                    P.dma("sync", xt[:, 8 * h2:8 * h2 + 8, :], xT[8 * h2:8 * h2 + 8, :, tok0:tok0 + TB].rearrange("c p t -> p c t"), writes=[xn])
                return
            if tok0 < CTXL:
                pieces = [(r, 0, QC, (r * QC) - tok0) for r in range(4) if tok0 <= r * QC < tok0 + TB]
            else:
                lt = tok0 - CTXL
                pieces = [(lt // QL, QC + lt % QL, TB, 0)]
            for (r, src0, n, dst0) in pieces:
                for k in range(4):
                    P.dma("sync", xt[:, 4 * k:4 * k + 4, dst0:dst0 + n], xg[k].ap()[4 * r:4 * r + 4, :, src0:src0 + n].rearrange("c p t -> p c t"),
                          reads=["xg%d" % k], writes=[xn])
                    P.dma("sync", xalt[:, 4 * k:4 * k + 4, dst0:dst0 + n], xg[k].ap()[16 + 4 * r:16 + 4 * r + 4, :, src0:src0 + n].rearrange("c p t -> p c t"),
                          reads=["xg%d" % k], writes=["xalt"])
            P.op("vector", lambda E: E.tensor_scalar(out=xt[:], in0=xt[:], scalar1=bsel[:, 0:1], scalar2=None, op0=ALU.mult), [xn, "bsel"], [xn])
            P.op("vector", lambda E: E.scalar_tensor_tensor(out=xt[:], in0=xalt[:], scalar=bsel[:, 1:2], in1=xt[:], op0=ALU.mult, op1=ALU.add),
                 ["xalt", "bsel", xn], [xn])

        with contextlib.ExitStack() as st:
            sb = lambda n, s, d: st.enter_context(nc.sbuf_tensor(n, s, d))
            wb = sb("wb", [128, DC, GCOLS], BF16)
            xts = [sb("xt%d" % i, [128, DC, TB], F32) for i in range(2)]
            if cc:
                xalt = sb("xalt", [128, DC, TB], F32)
                bsel = sb("bsel_s", [128, 2], F32)
                P.dma("sync", bsel[:], bsel_d, writes=["bsel"])
            sq = sb("sq", [128, DC, TB], BF16)
            hl = sb("hl", [128, DC, TB], BF16)
            rs = sb("rs", [128, TB], F32)
            stg = [sb("stg%d" % i, [128, 512], F32) for i in range(4)]
            vst = [sb("vst%d" % i, [128, 256], BF16) for i in range(2)]
            abst = [sb("abst%d" % i, [128, 512], BF16) for i in range(2)]
            uT = sb("uT", [128, 2, TB], BF16)
            dtmp = sb("dtmp", [128, 16], F32)
            for k in range(4):
                P.dma("gpsimd", wb[:, 4 * k:4 * k + 4, :], wg[4 * k:4 * k + 4].rearrange("c p n -> p c n"),
                      writes=["wb"])
            nstg = 0
            nft = 0
            nzt = 0
            for bi in range(LT // TB):
                tok0 = bi * TB
                isctx = tok0 < CTXL
                xt = xts[bi % 2]
                xn = "xt%d" % (bi % 2)
                load_x(xt, xn, tok0)
                S(lambda E, xt=xt: E.activation(out=sq[:], in_=xt[:], func=AF.Square), [xn], ["sq"])
                for c in range(DC):
                    T(lambda E, c=c: E.matmul(pb[0][:, :TB], lhsT=onesb[:], rhs=sq[:, c, :], start=(c == 0), stop=(c == DC - 1)), ["onesb", "sq"], ["pb0"])
                S(lambda E: E.activation(out=rs[:], in_=pb[0][:, :TB], func=AF.Sqrt, bias=epsb[:, 0:1], scale=1.0 / D), ["pb0", "epsb"], ["rs"])
                V(lambda E: E.reciprocal(out=rs[:], in_=rs[:]), ["rs"], ["rs"])
                s1 = s1c if isctx else s1l
                s1n = "s1c" if isctx else "s1l"
                shv = vec[:, 4, :] if isctx else vec[:, 2, :]
                V(lambda E, xt=xt: E.tensor_tensor(out=xt[:], in0=xt[:], in1=_bc(rs[:], [128, DC, TB], 1), op=ALU.mult), [xn, "rs"], [xn])
                G(lambda E, xt=xt, s1=s1: E.tensor_tensor(out=xt[:], in0=xt[:], in1=_bc(s1[:], [128, DC, TB], 2), op=ALU.mult), [xn, s1n], [xn])
                V(lambda E, xt=xt, shv=shv: E.tensor_tensor(out=hl[:], in0=xt[:], in1=_bc(shv, [128, DC, TB], 2), op=ALU.add), [xn, "vec"], ["hl"])
                for i in range(8):
                    col0 = 128 * i if i < 6 else 1280 + 128 * (i - 6)
                    pbi = 1 + (nft % 2)
                    nft += 1
                    for c in range(DC):
                        T(lambda E, c=c, col0=col0, pbi=pbi: E.matmul(pb[pbi][:, :TB], lhsT=wb[:, c, col0:col0 + 128], rhs=hl[:, c, :],
                                                                  start=(c == 0), stop=(c == DC - 1)), ["wb", "hl"], ["pb%d" % pbi])
                    if i < 6:
                        sg = stg[nstg % 4]
                        sgn = "stg%d" % (nstg % 4)
                        nstg += 1
                        S(lambda E, sg=sg, pbi=pbi: E.copy(out=sg[:, :TB], in_=pb[pbi][:, :TB]), ["pb%d" % pbi], [sgn])
                        P.dma("sync", s_xbc[i, :, tok0:tok0 + TB], sg[:, :TB], reads=[sgn], key=sgn)
                    else:
                        V(lambda E, i=i, pbi=pbi: E.tensor_copy(out=uT[:, i - 6, :], in_=pb[pbi][:, :TB]), ["pb%d" % pbi], ["uT"])
                for j in range(TB // 128):
                    ch = tok0 // 128 + j
                    tsl = slice(128 * j, 128 * j + 128)
                    pz = 3 + (nzt % 2)
                    nzt += 1
                    for c in range(DC):
                        T(lambda E, c=c, tsl=tsl, pz=pz: E.matmul(pb[pz][:, :], lhsT=hl[:, c, tsl], rhs=wb[:, c, 768:1280], start=(c == 0), stop=(c == DC - 1)),
                          ["wb", "hl"], ["pb%d" % pz])
                    sg = stg[nstg % 4]
                    sgn = "stg%d" % (nstg % 4)
                    nstg += 1
                    S(lambda E, sg=sg, pz=pz: E.activation(out=sg[:], in_=pb[pz][:], func=AF.Silu), ["pb%d" % pz], [sgn])
                    P.dma("sync", s_sz[128 * ch:128 * ch + 128, :], sg[:], reads=[sgn], key=sgn)
                    for c in range(DC):
                        T(lambda E, c=c, tsl=tsl: E.matmul(pb[5][:, 0:272], lhsT=hl[:, c, tsl], rhs=wb[:, c, 1536:1808], start=(c == 0), stop=(c == DC - 1)),
                          ["wb", "hl"], ["pb5"])
                    vs = vst[ch % 2]
                    vsn = "vst%d" % (ch % 2)
                    V(lambda E, vs=vs: E.tensor_copy(out=vs[:], in_=pb[5][:, 0:256]), ["pb5"], [vsn])
                    P.dma("sync", s_v[ch], vs[:], reads=[vsn], key=vsn)
                    V(lambda E: E.tensor_tensor(out=dtmp[:], in0=pb[5][:, 256:272], in1=tk[:, 0:16], op=ALU.add), ["pb5", "tk"], ["dtmp"])
                    S(lambda E: E.activation(out=dtmp[:], in_=dtmp[:], func=AF.Exp), ["dtmp"], ["dtmp"])
                    S(lambda E, ch=ch: E.activation(out=dt_all[:, ch, :], in_=dtmp[:], func=AF.Ln, bias=1.0), ["dtmp"], ["dt%d" % ch])
                    for kc in range(2):
                        T(lambda E, kc=kc, tsl=tsl: E.matmul(pb[6][:, :], lhsT=uT[:, kc, tsl], rhs=ccb[:, kc, :], start=(kc == 0), stop=(kc == 1)),
                          ["uT", "ccb"], ["pb6"])
                    as_ = abst[ch % 2]
                    asn = "abst%d" % (ch % 2)
                    V(lambda E, as_=as_: E.tensor_copy(out=as_[:], in_=pb[6][:, :]), ["pb6"], [asn])
                    P.dma("sync", s_ab[ch], as_[:], reads=[asn], key=asn)
            P.flush()

        with contextlib.ExitStack() as st23:
            xbcT = st23.enter_context(nc.sbuf_tensor("xbcT", [128, 6, LT], BF16))
            with contextlib.ExitStack() as st:
                sb = lambda n, s, d: st.enter_context(nc.sbuf_tensor(n, s, d))
                cins = [sb("cin%d" % i, [128, 516], F32) for i in range(2)]
                accs = [sb("acc%d" % i, [128, 512], F32) for i in range(2)]
                nci = 0
                cblocks = [(0, CTXL, True)] + [(CTXL + 512 * i, 512, False) for i in range(NLB)]
                for (tok0, CB, isctx) in cblocks:
                    seq_lo = 0 if isctx else CTXL
                    seq_hi = CTXL if isctx else LT
                    lo = max(seq_lo, tok0 - 2)
                    hi = min(seq_hi, tok0 + CB + 2)
                    off = lo - (tok0 - 2)
                    for i in range(6):
                        cin = cins[nci % 2]
                        cn = "cin%d" % (nci % 2)
                        acc = accs[nci % 2]
                        an = "acc%d" % (nci % 2)
                        nci += 1
                        G(lambda E, cin=cin: E.memset(cin[:], 0.0), [], [cn])
                        P.dma("sync", cin[:, off:off + hi - lo], s_xbc[i, :, lo:hi], writes=[cn])
                        V(lambda E, cin=cin, acc=acc, i=i, CB=CB: E.tensor_scalar(out=acc[:, :CB], in0=cin[:, 0:CB], scalar1=cw[:, i, 0:1], scalar2=cw[:, i, 5:6],
                                                                              op0=ALU.mult, op1=ALU.add), [cn, "cw"], [an])
                        for k in range(1, 5):
                            V(lambda E, cin=cin, acc=acc, i=i, CB=CB, k=k: E.scalar_tensor_tensor(out=acc[:, :CB], in0=cin[:, k:k + CB], scalar=cw[:, i, k:k + 1],
                                                                                               in1=acc[:, :CB], op0=ALU.mult, op1=ALU.add), [cn, "cw", an], [an])
                        S(lambda E, acc=acc, i=i, CB=CB, tok0=tok0: E.activation(out=xbcT[:, i, tok0:tok0 + CB], in_=acc[:, :CB], func=AF.Silu), [an], ["xbcT"])
                if dbg:
                    P.dma("sync", dbg_xbc.rearrange("c p t -> p c t"), xbcT[:], reads=["xbcT"], key="dbg")
                    P.dma("sync", dbg_dt, dt_all[:], key="dbg")
                P.flush()

            with contextlib.ExitStack() as st:
                sb = lambda n, s, d: st.enter_context(nc.sbuf_tensor(n, s, d))
                xs_all = sb("xs_all", [128, NCH, 512], BF16)
                b_all = sb("b_all", [128, NCH, 128], BF16)
                da_all = sb("da_all", [128, NCH, 16], F32)
                ex_all = sb("ex_all", [128, NCH, 48], F32)
                wdt_all = sb("wdt_all", [128, NCH, 16], F32)
                hbp_all = sb("hbp_all", [128, NCH, 512], BF16)
                arg = sb("arg", [128, 48], F32)
                Hf = sb("Hf", [128, 512], F32)
                Hb = sb("Hb", [128, 512], F32)
                hfb = [sb("hfb%d" % i, [128, 512], BF16) for i in range(2)]
                xdw = [sb("xdw%d" % i, [128, 512], BF16) for i in range(2)]
                pT = pb[7][:].bitcast(BF16)
                for ch in range(NCH):
                    csl = slice(128 * ch, 128 * ch + 128)
                    for m in range(4):
                        T(lambda E, m=m, csl=csl: E.transpose(pT[:, 128 * m:128 * m + 128], xbcT[:, m, csl], identb[:]), ["identb"], ["pb7"])
                    T(lambda E, csl=csl: E.transpose(pT[:, 512:640], xbcT[:, 4, csl], identb[:]), ["identb"], ["pb7"])
                    V(lambda E, ch=ch: E.tensor_copy(out=xs_all[:, ch, :], in_=pT[:, 0:512]), ["pb7"], ["xs%d" % ch])
                    V(lambda E, ch=ch: E.tensor_copy(out=b_all[:, ch, :], in_=pT[:, 512:640]), ["pb7"], ["bt%d" % ch])
                    V(lambda E, ch=ch: E.tensor_tensor(out=da_all[:, ch, :], in0=dt_all[:, ch, :], in1=aneg[:], op=ALU.mult), ["aneg"], ["da%d" % ch])
                    T(lambda E, ch=ch: E.matmul(pb[6][:, 0:8], lhsT=mk[:, 2, :], rhs=da_all[:, ch, 0:8], start=True, stop=True), ["mk", "da%d" % ch], ["pb6"])
                    T(lambda E, ch=ch: E.matmul(pb[6][:, 8:16], lhsT=mk[:, 3, :], rhs=da_all[:, ch, 8:16], start=True, stop=True), ["mk", "da%d" % ch], ["pb6"])
                    T(lambda E, ch=ch: E.matmul(pb[6][:, 16:32], lhsT=mk[:, 5, :], rhs=da_all[:, ch, 0:16], start=True, stop=True), ["mk", "da%d" % ch], ["pb6"])
                    V(lambda E: E.tensor_copy(out=arg[:, 0:16], in_=pb[6][:, 0:16]), ["pb6"], ["arg"])
                    V(lambda E: E.tensor_copy(out=arg[:, 32:48], in_=pb[6][:, 16:32]), ["pb6"], ["arg"])
                    V(lambda E: E.tensor_tensor(out=arg[:, 16:32], in0=arg[:, 32:48], in1=arg[:, 0:16], op=ALU.subtract), ["arg"], ["arg"])
                    S(lambda E, ch=ch: E.activation(out=ex_all[:, ch, :], in_=arg[:], func=AF.Exp), ["arg"], ["ex%d" % ch])
                    V(lambda E, ch=ch: E.tensor_tensor(out=wdt_all[:, ch, :], in0=dt_all[:, ch, :], in1=ex_all[:, ch, 16:32], op=ALU.mult),
                      ["ex%d" % ch], ["wdt%d" % ch])
                V(lambda E: E.memset(Hf[:], 0.0), [], ["Hf"])
                V(lambda E: E.memset(Hb[:], 0.0), [], ["Hb"])

                def state_update(H, Hn, ch, d0, k):
                    xw = xdw[k % 2]
                    xwn = "xdw%d" % (k % 2)
                    V(lambda E: E.tensor_tensor(out=h8(xw[:]), in0=h8(xs_all[:, ch, :]), in1=_bc(wdt_all[:, ch, d0:d0 + 8], [128, 8, 64], 2), op=ALU.mult),
                      ["xs%d" % ch, "wdt%d" % ch], [xwn])
                    T(lambda E: E.matmul(pb[5][:, :], lhsT=b_all[:, ch, :], rhs=xw[:], start=True, stop=True), ["bt%d" % ch, xwn], ["pb5"])
                    V(lambda E: E.tensor_tensor(out=h8(H[:]), in0=h8(H[:]), in1=_bc(ex_all[:, ch, 32 + d0:40 + d0], [128, 8, 64], 2), op=ALU.mult),
                      [Hn, "ex%d" % ch], [Hn])
                    V(lambda E: E.tensor_tensor(out=H[:], in0=H[:], in1=pb[5][:, :], op=ALU.add), [Hn, "pb5"], [Hn])

                border = list(range(NCC - 1, -1, -1)) + list(range(NCH - 1, NCC - 1, -1))
                for k, ch in enumerate(border):
                    S(lambda E, ch=ch: E.copy(out=hbp_all[:, ch, :], in_=Hb[:]), ["Hb"], ["hbp%d" % ch])
                    state_update(Hb, "Hb", ch, 8, k)

                cbm = sb("cbm", [128, 2, 128], F32)
                RF = sb("RF", [128, 8, 128], F32)
                RB = sb("RB", [128, 8, 128], F32)
                Ee = sb("Ee", [128, 1024], F32)
                mtF = sb("mtF", [128, 8, 128], BF16)
                mtB = sb("mtB", [128, 8, 128], BF16)
                xdF = sb("xdF", [128, 512], BF16)
                xdB = sb("xdB", [128, 512], BF16)
                tA = sb("tA", [128, 512], F32)
                tB = sb("tB", [128, 512], F32)
                szt = [sb("szt%d" % i, [128, 512], F32) for i in range(2)]
                ss = sb("ss", [128, 2], F32)
                ynb = [sb("ynb%d" % i, [128, 512], BF16) for i in range(2)]
                for ch in range(NCH):
                    csl = slice(128 * ch, 128 * ch + 128)
                    hf_b = hfb[ch % 2]
                    hfn = "hfb%d" % (ch % 2)
                    S(lambda E, hf_b=hf_b: E.copy(out=hf_b[:], in_=Hf[:]), ["Hf"], [hfn])
                    T(lambda E, csl=csl: E.matmul(pb[0][:, 0:128], lhsT=xbcT[:, 4, csl], rhs=xbcT[:, 5, csl], start=True, stop=True), [], ["pb0"])
                    V(lambda E: E.tensor_tensor(out=cbm[:], in0=_bc(pb[0][:, 0:128], [128, 2, 128], 1), in1=mk[:, 2:4, :], op=ALU.mult), ["pb0", "mk"], ["cbm"])
                    G(lambda E, ch=ch: E.tensor_tensor(out=RF[:], in0=_bc(mk[:, 2, :], [128, 8, 128], 1), in1=_bc(da_all[:, ch, 0:8], [128, 8, 128], 2), op=ALU.mult),
                      ["mk", "da%d" % ch], ["RF"])
                    G(lambda E, ch=ch: E.tensor_tensor(out=RB[:], in0=_bc(mk[:, 3, :], [128, 8, 128], 1), in1=_bc(da_all[:, ch, 8:16], [128, 8, 128], 2), op=ALU.mult),
                      ["mk", "da%d" % ch], ["RB"])
                    for hh in range(2):
                        T(lambda E, hh=hh: E.matmul(pb[1 + hh][:, :], lhsT=mk[:, 0, :], rhs=RF[:, 4 * hh:4 * hh + 4, :].rearrange("p h l -> p (h l)"), start=True, stop=True),
                          ["mk", "RF"], ["pb%d" % (1 + hh)])
                        S(lambda E, hh=hh: E.activation(out=Ee[:, 512 * hh:512 * hh + 512], in_=pb[1 + hh][:, :], func=AF.Exp), ["pb%d" % (1 + hh)], ["Ee"])
                    V(lambda E: E.tensor_tensor(out=mtF[:], in0=Ee[:].rearrange("p (h l) -> p h l", h=8), in1=_bc(cbm[:, 0, :], [128, 8, 128], 1), op=ALU.mult),
                      ["Ee", "cbm"], ["mtF"])
                    for hh in range(2):
                        T(lambda E, hh=hh: E.matmul(pb[1 + hh][:, :], lhsT=mk[:, 1, :], rhs=RB[:, 4 * hh:4 * hh + 4, :].rearrange("p h l -> p (h l)"), start=True, stop=True),
                          ["mk", "RB"], ["pb%d" % (1 + hh)])
                        S(lambda E, hh=hh: E.activation(out=Ee[:, 512 * hh:512 * hh + 512], in_=pb[1 + hh][:, :], func=AF.Exp), ["pb%d" % (1 + hh)], ["Ee"])
                    V(lambda E: E.tensor_tensor(out=mtB[:], in0=Ee[:].rearrange("p (h l) -> p h l", h=8), in1=_bc(cbm[:, 1, :], [128, 8, 128], 1), op=ALU.mult),
                      ["Ee", "cbm"], ["mtB"])
                    V(lambda E, ch=ch: E.tensor_tensor(out=h8(xdF[:]), in0=h8(xs_all[:, ch, :]), in1=_bc(dt_all[:, ch, 0:8], [128, 8, 64], 2), op=ALU.mult),
                      ["xs%d" % ch], ["xdF"])
                    V(lambda E, ch=ch: E.tensor_tensor(out=h8(xdB[:]), in0=h8(xs_all[:, ch, :]), in1=_bc(dt_all[:, ch, 8:16], [128, 8, 64], 2), op=ALU.mult),
                      ["xs%d" % ch], ["xdB"])
                    for h in range(8):
                        hs = slice(64 * h, 64 * h + 64)
                        T(lambda E, h=h, hs=hs: E.matmul(pb[3][:, hs], lhsT=mtF[:, h, :], rhs=xdF[:, hs], start=True, stop=False), ["mtF", "xdF"], ["pb3"])
                        T(lambda E, h=h, hs=hs: E.matmul(pb[3][:, hs], lhsT=mtB[:, h, :], rhs=xdB[:, hs], start=False, stop=True), ["mtB", "xdB"], ["pb3"])
                    T(lambda E, csl=csl, hf_b=hf_b: E.matmul(pb[4][:, :], lhsT=xbcT[:, 5, csl], rhs=hf_b[:], start=True, stop=True), [hfn], ["pb4"])
                    T(lambda E, csl=csl, ch=ch: E.matmul(pb[6][:, :], lhsT=xbcT[:, 5, csl], rhs=hbp_all[:, ch, :], start=True, stop=True), ["hbp%d" % ch], ["pb6"])
                    V(lambda E, ch=ch: E.tensor_tensor(out=h8(tA[:]), in0=h8(pb[4][:, :]), in1=_bc(ex_all[:, ch, 0:8], [128, 8, 64], 2), op=ALU.mult),
                      ["pb4", "ex%d" % ch], ["tA"])
                    V(lambda E, ch=ch: E.tensor_tensor(out=h8(tB[:]), in0=h8(pb[6][:, :]), in1=_bc(ex_all[:, ch, 8:16], [128, 8, 64], 2), op=ALU.mult),
                      ["pb6", "ex%d" % ch], ["tB"])
                    G(lambda E: E.tensor_tensor(out=tA[:], in0=tA[:], in1=tB[:], op=ALU.add), ["tA", "tB"], ["tA"])
                    V(lambda E: E.tensor_tensor(out=tA[:], in0=tA[:], in1=pb[3][:, :], op=ALU.add), ["tA", "pb3"], ["tA"])
                    G(lambda E, ch=ch: E.tensor_tensor(out=tB[:], in0=xs_all[:, ch, :], in1=tk[:, 32:544], op=ALU.mult), ["xs%d" % ch, "tk", "tB"], ["tB"])
                    V(lambda E: E.tensor_tensor(out=tA[:], in0=tA[:], in1=tB[:], op=ALU.add), ["tA", "tB"], ["tA"])
                    if dbg:
                        P.dma("sync", dbg_y[csl, :], tA[:], reads=["tA"], key="dbg")
                    sz = szt[ch % 2]
                    szn = "szt%d" % (ch % 2)
                    P.dma("sync", sz[:], s_sz[csl, :], writes=[szn])
                    V(lambda E, sz=sz: E.tensor_tensor(out=tA[:], in0=tA[:], in1=sz[:], op=ALU.mult), ["tA", szn], ["tA"])
                    S(lambda E: E.activation(out=tB[:], in_=tA[:], func=AF.Square, accum_out=ss[:, 0:1]), ["tA", "tB"], ["tB", "ss"])
                    S(lambda E: E.activation(out=ss[:, 1:2], in_=ss[:, 0:1], func=AF.Sqrt, bias=epsb[:, 0:1], scale=1.0 / 512), ["ss", "epsb"], ["ss"])
                    V(lambda E: E.reciprocal(out=ss[:, 1:2], in_=ss[:, 1:2]), ["ss"], ["ss"])
                    yb = ynb[ch % 2]
                    ybn = "ynb%d" % (ch % 2)
                    V(lambda E, yb=yb: E.scalar_tensor_tensor(out=yb[:], in0=tA[:], scalar=ss[:, 1:2], in1=tk[:, 544:1056], op0=ALU.mult, op1=ALU.mult),
                      ["tA", "ss", "tk"], [ybn])
                    P.dma("sync", yn_o[csl, :], yb[:], reads=[ybn], key=ybn)
                    state_update(Hf, "Hf", ch, 0, ch)
                P.flush()

        with contextlib.ExitStack() as st:
            sb = lambda n, s, d: st.enter_context(nc.sbuf_tensor(n, s, d))
            v_all = sb("v_all", [128, NCH, 256], BF16)
            ab_all = sb("ab_all", [128, NCH, 512], BF16)
            pbs = [sb("pbs%d" % i, [128, 12, 512], BF16) for i in range(2)]
            dfs = [sb("dfs%d" % i, [128, 8, 2, 512], BF16) for i in range(3)]
            ost = [sb("ost%d" % i, [128, 512], BF16) for i in range(4)]
            pcs = sb("pcs", [128, NCC, CTXL], BF16)
            dcs = sb("dcs", [128, NCC, 2, CTXL], BF16)
            nost = 0
            P.dma("sync", v_all[:], s_v.rearrange("c p t -> p c t"), writes=["v_all"])
            P.dma("sync", ab_all[:], s_ab.rearrange("c p t -> p c t"), writes=["ab_all"])
            P.dma("sync", pcs[:], pctx.rearrange("c p t -> p c t"), writes=["pcs"])
            P.dma("sync", dcs[:], dftC.rearrange("c p a t -> p c a t"), writes=["dcs"])
            for m in range(2):
                for c in range(NCC):
                    T(lambda E, m=m, c=c: E.matmul(pb[1][:, :CTXL], lhsT=v_all[:, c, 128 * m:128 * m + 128], rhs=pcs[:, c, :], start=(c == 0), stop=(c == NCC - 1)),
                      ["v_all", "pcs"], ["pb1"])
                o = ost[nost % 4]
                on = "ost%d" % (nost % 4)
                nost += 1
                V(lambda E, o=o: E.tensor_copy(out=o[:, :CTXL], in_=pb[1][:, :CTXL]), ["pb1"], [on])
                P.dma("sync", dT_o[m, :, 0:CTXL], o[:, :CTXL], reads=[on], key=on)
                n = 0
                for c in range(NCC):
                    for a in range(2):
                        T(lambda E, m=m, c=c, a=a, n=n: E.matmul(pb[2][:, :CTXL], lhsT=ab_all[:, c, 256 * a + 128 * m:256 * a + 128 * m + 128], rhs=dcs[:, c, a, :],
                                                                start=(n == 0), stop=(n == 2 * NCC - 1)), ["ab_all", "dcs"], ["pb2"])
                        n += 1
                o = ost[nost % 4]
                on = "ost%d" % (nost % 4)
                nost += 1
                V(lambda E, o=o: E.tensor_copy(out=o[:, :CTXL], in_=pb[2][:, :CTXL]), ["pb2"], [on])
                P.dma("sync", fT_o[m, :, 0:CTXL], o[:, :CTXL], reads=[on], key=on)
            for j in range(NLB):
                pbuf = pbs[j % 2]
                pn = "pbs%d" % (j % 2)
                P.dma("sync", pbuf[:], pband[j].rearrange("c p t -> p c t"), writes=[pn])
                clist = [c for c in range(4 * j - 4, 4 * j + 8) if 0 <= c < NLC]
                for m in range(2):
                    for n, c in enumerate(clist):
                        ci = c - (4 * j - 4)
                        T(lambda E, m=m, c=c, ci=ci, n=n, pbuf=pbuf, nn=len(clist): E.matmul(pb[1 + m][:, :], lhsT=v_all[:, NCC + c, 128 * m:128 * m + 128], rhs=pbuf[:, ci, :],
                                                                                             start=(n == 0), stop=(n == nn - 1)), ["v_all", pn], ["pb%d" % (1 + m)])
                    o = ost[nost % 4]
                    on = "ost%d" % (nost % 4)
                    nost += 1
                    V(lambda E, o=o, m=m: E.tensor_copy(out=o[:], in_=pb[1 + m][:, :]), ["pb%d" % (1 + m)], [on])
                    P.dma("sync", dT_o[m, :, CTXL + 512 * j:CTXL + 512 * j + 512], o[:], reads=[on], key=on)
            ndf = 0
            for j in range(NLB):
                for c8 in range(0, NLC, 8):
                    df = dfs[ndf % 3]
                    dn = "dfs%d" % (ndf % 3)
                    ndf += 1
                    nn = min(8, NLC - c8)
                    for a in range(2):
                        P.dma("sync", df[:, :nn, a, :], dftL[c8:c8 + nn, :, a, 512 * j:512 * j + 512].rearrange("c p t -> p c t"), writes=[dn])
                    for cc in range(nn):
                        c = c8 + cc
                        for a in range(2):
                            for m in range(2):
                                first = (c == 0 and a == 0)
                                last = (c == NLC - 1 and a == 1)
                                T(lambda E, m=m, c=c, cc=cc, a=a, df=df, first=first, last=last: E.matmul(
                                    pb[3 + m][:, :], lhsT=ab_all[:, NCC + c, 256 * a + 128 * m:256 * a + 128 * m + 128], rhs=df[:, cc, a, :], start=first, stop=last),
                                  ["ab_all", dn], ["pb%d" % (3 + m)])
                for m in range(2):
                    o = ost[nost % 4]
                    on = "ost%d" % (nost % 4)
                    nost += 1
                    S(lambda E, o=o, m=m: E.copy(out=o[:], in_=pb[3 + m][:, :]), ["pb%d" % (3 + m)], [on])
                    P.dma("sync", fT_o[m, :, CTXL + 512 * j:CTXL + 512 * j + 512], o[:], reads=[on], key=on)
            P.flush(barrier=False)
    return nc


def _box_matrix(n, w):
    idx = np.arange(n)
    lo = np.clip(idx - w // 2, 0, n)
    hi = np.clip(idx + (w - w // 2), 0, n)
    M = np.zeros((n, n), np.float64)
    for o in range(n):
        M[o, lo[o]:hi[o]] = 1.0 / (hi[o] - lo[o])
    return M


def dft_table(L):
    t = np.arange(L)
    a = 2 * np.pi * (np.outer(t, t) % L) / L
    sc = 1.0 / np.sqrt(L * 256.0)
    m = np.stack([np.cos(a) * sc, -np.sin(a) * sc], 1)
    return np.ascontiguousarray(m.reshape(L // 128, 128, 2, L)).astype(NPBF)


def mixer_consts(LAT, CTXL, w, with_dft=True):
    rows = LAT // 64
    c = {}
    k = np.arange(128)
    U = (k[:, None] > k[None, :])
    Lo = (k[:, None] < k[None, :])
    Tri = (k[:, None] <= k[None, :])
    TriB = (k[:, None] >= k[None, :])
    c["masks"] = np.stack([U, Lo, Tri, TriB, np.eye(128, dtype=bool), np.ones((128, 128), bool)], 1).astype(np.float32)
    c["identb"] = np.eye(128, dtype=np.float32).astype(NPBF)
    q = np.arange(256)
    ang = 2 * np.pi * np.outer(q, q) / 256
    cc = np.concatenate([np.cos(ang), np.sin(ang)], 1)
    c["ccsc"] = np.ascontiguousarray(cc.reshape(2, 128, 512).transpose(1, 0, 2)).astype(NPBF)

    if with_dft:
        c["dftL"] = dft_table(LAT)
    c["dftC"] = dft_table(CTXL)
    Rr = _box_matrix(rows, w)
    Rc = _box_matrix(64, w)
    Pm = np.kron(Rr, Rc) - np.eye(LAT)
    PT = Pm.T
    NLB = LAT // 512
    pband = np.zeros((NLB, 12, 128, 512), np.float32)
    for j in range(NLB):
        for ci in range(12):
            ch = 4 * j - 4 + ci
            if 0 <= ch < LAT // 128:
                pband[j, ci] = PT[128 * ch:128 * ch + 128, 512 * j:512 * j + 512]
    c["pband"] = pband.astype(NPBF)
    P1 = (_box_matrix(CTXL, w) - np.eye(CTXL)).T
    c["pctx"] = np.ascontiguousarray(P1.reshape(CTXL // 128, 128, CTXL)).astype(NPBF)
    return c


Z_OFF = 3136
POOL_OFF = 5184
FOUR_OFF = 6208
GATE_OFF = 7232
POOL_WINDOWS = (2, 4, 8, 16)


def _group_cols(g):
    r = np.arange
    return np.concatenate([
        g * 512 + r(512), 2048 + g * 128 + r(128), 2560 + g * 128 + r(128),
        Z_OFF + g * 512 + r(512), FOUR_OFF + g * 256 + r(256), POOL_OFF + g * 256 + r(256),
        3072 + g * 8 + r(8), 3104 + g * 8 + r(8)])


def _fm(v):
    return np.ascontiguousarray(np.asarray(v, np.float32).reshape(DC, 128).T)


def mixer_in_map(xt, w_in_l, nw, sc, sh, csc, csh, conv_w_l, conv_b_l, a_log_l, dt_bias_l, d_skip_l,
                 ssd_norm_w_l, g, consts):
    cols = _group_cols(g)
    wgm = np.ascontiguousarray(w_in_l[:, cols]).reshape(DC, 128, GCOLS)
    vecs = np.stack([_fm(nw), _fm(sc), _fm(sh), _fm(csc), _fm(csh)], 1)
    ccols = cols[:768]
    cp = np.concatenate([conv_w_l[:, ccols].T, conv_b_l[ccols][:, None]], 1)
    convp = np.ascontiguousarray(cp.reshape(6, 128, 6).transpose(1, 0, 2))
    hsl = slice(g * 8, g * 8 + 8)
    row = np.concatenate([dt_bias_l[0, hsl], dt_bias_l[1, hsl], a_log_l[0, hsl], a_log_l[1, hsl],
                          np.repeat(d_skip_l[hsl], 64), ssd_norm_w_l[g * 512:(g + 1) * 512]]).astype(np.float32)
    tokc = np.ascontiguousarray(np.broadcast_to(row, (128, 1056)))
    m = dict(wg=wgm, vecs=np.ascontiguousarray(vecs), convp=convp, tokc=tokc)
    if xt is not None:
        m["xT"] = np.ascontiguousarray(xt, dtype=np.float32)
    m.update(consts)
    return m


WSPEC = [("wgt", 48, DC, 3), ("wsso", 16, DC, 2), ("wpool", 16, 2, 2), ("wfour", 16, 8, 2), ("wout", 16, DC, 2),
         ("wfg", 48, DC, 3), ("wfu", 48, DC, 3), ("wfd", 16, DFF // 128, 1)]


def build_dense(nblk, LB, CB, last, cc=False):
    NTb = LB + CB
    NT = nblk * NTb
    segs = [(0, LB, 0)] + ([(LB, CB, 1)] if CB else [])
    NS = len(segs)
    NF = DFF // 128
    nc = bass.Bass("TRN2", target_bir_lowering=False)
    din = lambda n, s, d: nc.dram_tensor(n, s, d, kind="ExternalInput").ap()
    dout = lambda n, s, d: nc.dram_tensor(n, s, d, kind="ExternalOutput").ap()
    xin = din("xin", [DC, 128, NT], F32)
    yT = din("yT", [DC, 128, NT], BF16)
    dT = din("dT", [8, 128, NT], BF16)
    fT = din("fT", [8, 128, NT], BF16)
    mods = din("mods", [128, 2, 8, DC], F32)
    misc = din("misc", [128, 2, DC], F32)
    WT = {}
    WCC = []
    for (wn_, nt_, kc_, tpc_) in WSPEC:
        if cc:
            sh = din(wn_, [nt_ // 8, 128, kc_, 128], F32)
            bo = nc.dram_tensor(wn_ + "_b", [nt_ // 8, 128, kc_, 128], F32)
            ga = nc.dram_tensor(wn_ + "_g", [nt_, 128, kc_, 128], F32)
            WCC.append((wn_, sh, bo, ga, nt_, tpc_))
            WT[wn_] = ga.ap()
        else:
            WT[wn_] = din(wn_, [NF if wn_ in ("wfg", "wfu") else nt_, 128, kc_, 128], F32)
    wgt, wsso, wpool, wfour, wout, wfg, wfu, wfd = [WT[k[0]] for k in WSPEC]
    xo = dout("xo", [DC, 128, NT], F32)

    with contextlib.ExitStack() as glob:
        gsb = lambda n, s, d: glob.enter_context(nc.sbuf_tensor(n, s, d))
        P = Prog(nc, glob)
        V = lambda fn, r, w: P.op("vector", fn, r, w)
        S = lambda fn, r, w: P.op("scalar", fn, r, w)
        T = lambda fn, r, w: P.op("tensor", fn, r, w)
        pb = [glob.enter_context(nc.psum_tensor("pb%d" % i, [128, 512], F32)) for i in range(8)]
        md = gsb("md", [128, 2, 8, DC], F32)
        mc = gsb("mc", [128, 2, DC], F32)
        s1 = gsb("s1", [128, 2, 2, DC], F32)
        onesb = gsb("onesb", [128, 128], BF16)
        epsb = gsb("epsb", [128, 1], F32)
        x = gsb("x", [128, DC, NTb], F32)
        hl = gsb("hl", [128, DC, NTb], BF16)
        sq = gsb("sq", [128, DC, NTb], BF16)
        rs = gsb("rs", [128, NTb], F32)
        P.dma("sync", md[:], mods, writes=["md"])
        P.dma("sync", mc[:], misc, writes=["mc"])
        for (wn_, sh, bo, ga, nt_, tpc_) in WCC:
            P.dma("gpsimd", bo.ap(), sh, writes=[wn_ + "_b"])
            for j in range(nt_ // (8 * tpc_)):
                P.cc("AllGather", [bo.ap()[j * tpc_:(j + 1) * tpc_].opt()], [ga.ap()[j * 8 * tpc_:(j + 1) * 8 * tpc_].opt()], [list(range(8))],
                     reads=[wn_ + "_b"], writes=["wg_" + wn_], key="cc_%s%d" % (wn_, j))
        V(lambda E: E.memset(onesb[:], 1.0), [], ["onesb"])
        V(lambda E: E.memset(epsb[:], EPS), [], ["epsb"])
        for s_ in range(2):
            for q in range(2):
                V(lambda E, s_=s_, q=q: E.scalar_tensor_tensor(out=s1[:, s_, q, :], in0=md[:, s_, 4 * q + 1, :], scalar=1.0, in1=md[:, s_, 4 * q, :],
                                                           op0=ALU.add, op1=ALU.mult), ["md"], ["s1"])

        def norm(q, dst, dstn):
            S(lambda E: E.activation(out=sq[:], in_=x[:], func=AF.Square), ["x"], ["sq"])
            for (o, n, st_) in segs:
                bank = pb[7] if st_ else pb[6]
                bn = "pb7" if st_ else "pb6"
                for c in range(DC):
                    T(lambda E, c=c, o=o, n=n, bank=bank: E.matmul(bank[:, :n], lhsT=onesb[:], rhs=sq[:, c, o:o + n], start=(c == 0), stop=(c == DC - 1)),
                      ["onesb", "sq"], [bn])
                S(lambda E, o=o, n=n, bank=bank: E.activation(out=rs[:, o:o + n], in_=bank[:, :n], func=AF.Sqrt, bias=epsb[:, 0:1], scale=1.0 / D),
                  [bn, "epsb"], ["rs"])
            V(lambda E: E.reciprocal(out=rs[:], in_=rs[:]), ["rs"], ["rs"])
            for (o, n, st_) in segs:
                if q < 2:
                    sc_ap = s1[:, st_, q, :]
                    sh_ap = md[:, st_, 4 * q + 2, :]
                else:
                    sc_ap = mc[:, 1, :]
                    sh_ap = None
                V(lambda E, o=o, n=n: E.tensor_tensor(out=dst[:, :, o:o + n], in0=x[:, :, o:o + n], in1=_bc(rs[:, o:o + n], [128, DC, n], 1), op=ALU.mult),
                  ["x", "rs"], [dstn])
                V(lambda E, o=o, n=n, sc_ap=sc_ap: E.tensor_tensor(out=dst[:, :, o:o + n], in0=dst[:, :, o:o + n], in1=_bc(sc_ap, [128, DC, n], 2), op=ALU.mult),
                  [dstn, "s1", "mc"], [dstn])
                if sh_ap is not None:
                    V(lambda E, o=o, n=n, sh_ap=sh_ap: E.tensor_tensor(out=dst[:, :, o:o + n], in0=dst[:, :, o:o + n], in1=_bc(sh_ap, [128, DC, n], 2), op=ALU.add),
                      [dstn, "md"], [dstn])

        nw = [0]
        wdone = [False]

        def wload(pool, dram_tile, kc):
            i = nw[0] % len(pool)
            nw[0] += 1
            t, tn = pool[i]
            step = 16 if kc <= 16 else 11
            for k0 in range(0, kc, step):
                k1 = min(kc, k0 + step)
                P.dma("gpsimd", t[:, k0:k1, :], dram_tile[:, k0:k1, :],
                      reads=(["wg_" + dram_tile.tensor.name[:-2]] if (cc and not wdone[0]) else []), writes=[tn])
            return t, tn

        for blk in range(nblk):
            t0 = blk * NTb
            for h2 in range(2):
                P.dma("sync", x[:, 8 * h2:8 * h2 + 8, :], xin[8 * h2:8 * h2 + 8, :, t0:t0 + NTb].rearrange("c p t -> p c t"), writes=["x"])
            with contextlib.ExitStack() as st:
                sb = lambda n, s, d: st.enter_context(nc.sbuf_tensor("%s_m%d" % (n, blk), s, d))
                ys = sb("ys", [128, DC, NTb], BF16)
                ds_ = sb("ds", [128, 8, NTb], BF16)
                fs = sb("fs", [128, 8, NTb], BF16)
                mg = sb("mg", [128, DC, NTb], BF16)
                macc = sb("macc", [128, NTb], F32)
                sg = [sb("sg%d" % i, [128, NTb], F32) for i in range(2)]
                wp = [(sb("wp%d" % i, [128, DC, 128], BF16), "wp%d" % i) for i in range(8)]
                P.dma("sync", ys[:], yT[:, :, t0:t0 + NTb].rearrange("c p t -> p c t"), writes=["ys"])
                P.dma("sync", ds_[:], dT[:, :, t0:t0 + NTb].rearrange("c p t -> p c t"), writes=["ds"])
                P.dma("sync", fs[:], fT[:, :, t0:t0 + NTb].rearrange("c p t -> p c t"), writes=["fs"])
                norm(0, hl, "hl")
                nb = 0
                for n in range(16):
                    for br in range(3):
                        wg_t, wg_n = wload(wp, wgt[br * 16 + n], DC)
                        if br == 0:
                            wb_t, wb_n = wload(wp, wpool[n], 2)
                            src, srcn, kc, koff = ds_, "ds", 2, 2 * (n // 4)
                        elif br == 1:
                            wb_t, wb_n = wload(wp, wfour[n], 8)
                            src, srcn, kc, koff = fs, "fs", 8, 0
                        else:
                            wb_t, wb_n = wload(wp, wsso[n], DC)
                            src, srcn, kc, koff = ys, "ys", DC, 0
                        par = nb % 2
                        nb += 1
                        for si, (o, ln, st_) in enumerate(segs):
                            gb = 4 * par + si
                            yb = 4 * par + 2 + si
                            for c in range(DC):
                                T(lambda E, c=c, o=o, ln=ln, gb=gb, wg_t=wg_t: E.matmul(pb[gb][:, :ln], lhsT=wg_t[:, c, :], rhs=hl[:, c, o:o + ln],
                                                                                    start=(c == 0), stop=(c == DC - 1)), [wg_n, "hl"], ["pb%d" % gb])
                            for c in range(kc):
                                T(lambda E, c=c, o=o, ln=ln, yb=yb, wb_t=wb_t, src=src, koff=koff, kc=kc: E.matmul(
                                    pb[yb][:, :ln], lhsT=wb_t[:, c, :], rhs=src[:, koff + c, o:o + ln], start=(c == 0), stop=(c == kc - 1)),
                                  [wb_n, srcn], ["pb%d" % yb])
                            sgt = sg[si]
                            sgn = "sg%d" % si
                            S(lambda E, o=o, ln=ln, gb=gb, sgt=sgt: E.activation(out=sgt[:, o:o + ln], in_=pb[gb][:, :ln], func=AF.Sigmoid), ["pb%d" % gb], [sgn])
                            if br == 0:
                                V(lambda E, o=o, ln=ln, yb=yb, sgt=sgt, n=n: E.scalar_tensor_tensor(out=macc[:, o:o + ln], in0=pb[yb][:, :ln], scalar=mc[:, 0, n:n + 1],
                                                                                                in1=sgt[:, o:o + ln], op0=ALU.mult, op1=ALU.mult),
                                  ["pb%d" % yb, sgn, "mc"], ["macc"])
                            else:
                                V(lambda E, o=o, ln=ln, yb=yb, sgt=sgt: E.tensor_tensor(out=sgt[:, o:o + ln], in0=sgt[:, o:o + ln], in1=pb[yb][:, :ln], op=ALU.mult),
                                  ["pb%d" % yb, sgn], [sgn])
                                if br == 1:
                                    V(lambda E, o=o, ln=ln, sgt=sgt: E.tensor_tensor(out=macc[:, o:o + ln], in0=macc[:, o:o + ln], in1=sgt[:, o:o + ln], op=ALU.add),
                                      ["macc", sgn], ["macc"])
                                else:
                                    V(lambda E, o=o, ln=ln, sgt=sgt, n=n: E.tensor_tensor(out=mg[:, n, o:o + ln], in0=macc[:, o:o + ln], in1=sgt[:, o:o + ln], op=ALU.add),
                                      ["macc", sgn], ["mg"])
                for n in range(16):
                    w_t, w_n = wload(wp, wout[n], DC)
                    par = n % 2
                    for si, (o, ln, st_) in enumerate(segs):
                        ob = 2 * par + si
                        for c in range(DC):
                            T(lambda E, c=c, o=o, ln=ln, ob=ob, w_t=w_t: E.matmul(pb[ob][:, :ln], lhsT=w_t[:, c, :], rhs=mg[:, c, o:o + ln],
                                                                              start=(c == 0), stop=(c == DC - 1)), [w_n, "mg"], ["pb%d" % ob])
                        V(lambda E, o=o, ln=ln, ob=ob, n=n, st_=st_: E.scalar_tensor_tensor(out=x[:, n, o:o + ln], in0=pb[ob][:, :ln], scalar=md[:, st_, 3, n:n + 1],
                                                                                         in1=x[:, n, o:o + ln], op0=ALU.mult, op1=ALU.add),
                          ["pb%d" % ob, "md", "x"], ["x"])
                P.flush()
                wdone[0] = True
            with contextlib.ExitStack() as st:
                sb = lambda n, s, d: st.enter_context(nc.sbuf_tensor("%s_f%d" % (n, blk), s, d))
                hT = sb("hT", [128, NF, NTb], BF16)
                sg = [sb("sg%d" % i, [128, NTb], F32) for i in range(2)]
                wp = [(sb("wq%d" % i, [128, DC, 128], BF16), "wq%d" % i) for i in range(6)]
                wd = [(sb("wd%d" % i, [128, NF, 128], BF16), "wd%d" % i) for i in range(2)]
                norm(1, hl, "hl")
                for j in range(NF):
                    wg_t, wg_n = wload(wp, wfg[j], DC)
                    wu_t, wu_n = wload(wp, wfu[j], DC)
                    par = j % 2
                    for si, (o, ln, st_) in enumerate(segs):
                        gb = 4 * par + si
                        ub = 4 * par + 2 + si
                        for c in range(DC):
                            T(lambda E, c=c, o=o, ln=ln, gb=gb, wg_t=wg_t: E.matmul(pb[gb][:, :ln], lhsT=wg_t[:, c, :], rhs=hl[:, c, o:o + ln],
                                                                                start=(c == 0), stop=(c == DC - 1)), [wg_n, "hl"], ["pb%d" % gb])
                        for c in range(DC):
                            T(lambda E, c=c, o=o, ln=ln, ub=ub, wu_t=wu_t: E.matmul(pb[ub][:, :ln], lhsT=wu_t[:, c, :], rhs=hl[:, c, o:o + ln],
                                                                                start=(c == 0), stop=(c == DC - 1)), [wu_n, "hl"], ["pb%d" % ub])
                        sgt = sg[par]
                        sgn = "sg%d" % par
                        S(lambda E, o=o, ln=ln, gb=gb, sgt=sgt: E.activation(out=sgt[:, o:o + ln], in_=pb[gb][:, :ln], func=AF.Silu), ["pb%d" % gb], [sgn])
                        V(lambda E, o=o, ln=ln, ub=ub, sgt=sgt, j=j: E.tensor_tensor(out=hT[:, j, o:o + ln], in0=sgt[:, o:o + ln], in1=pb[ub][:, :ln], op=ALU.mult),
                          ["pb%d" % ub, sgn], ["hT"])
                for n in range(16):
                    w_t, w_n = wload(wd, wfd[n], NF)
                    par = n % 2
                    for si, (o, ln, st_) in enumerate(segs):
                        ob = 2 * par + si
                        for c in range(NF):
                            T(lambda E, c=c, o=o, ln=ln, ob=ob, w_t=w_t: E.matmul(pb[ob][:, :ln], lhsT=w_t[:, c, :], rhs=hT[:, c, o:o + ln],
                                                                              start=(c == 0), stop=(c == NF - 1)), [w_n, "hT"], ["pb%d" % ob])
                        V(lambda E, o=o, ln=ln, ob=ob, n=n, st_=st_: E.scalar_tensor_tensor(out=x[:, n, o:o + ln], in0=pb[ob][:, :ln], scalar=md[:, st_, 7, n:n + 1],
                                                                                         in1=x[:, n, o:o + ln], op0=ALU.mult, op1=ALU.add),
                          ["pb%d" % ob, "md", "x"], ["x"])
                if last:
                    norm(2, x, "x")
                if True:
                    for h2 in range(2):
                        P.dma("sync", xo[8 * h2:8 * h2 + 8, :, t0:t0 + NTb].rearrange("c p t -> p c t"), x[:, 8 * h2:8 * h2 + 8, :], reads=["x"], key="xo")
                P.flush(barrier=(blk < nblk - 1))
    return nc


def _tile_w(w, kc=None):
    K_, N_ = w.shape
    return np.ascontiguousarray(w.reshape(K_ // 128, 128, N_ // 128, 128).transpose(2, 1, 0, 3))


def shard_weights(W, r):
    out = {}
    for (wn_, nt_, kc_, tpc_) in WSPEC:
        a = W[wn_]
        if a.shape[0] < nt_:
            a = np.concatenate([a, np.zeros((nt_ - a.shape[0],) + a.shape[1:], a.dtype)], 0)
        idx = [(j * 8 + r) * tpc_ + t for j in range(nt_ // (8 * tpc_)) for t in range(tpc_)]
        out[wn_] = np.ascontiguousarray(a[idx])
    return out


def dense_weights(inp, l):
    w = {}
    w["wgt"] = _tile_w(inp["w_in"][l][:, GATE_OFF:])
    w["wsso"] = _tile_w(inp["w_ssd_out"][l])
    wp = inp["w_pool"][l]
    w["wpool"] = np.ascontiguousarray(np.concatenate([_tile_w(wp[g]) for g in range(4)], 0))
    w["wfour"] = _tile_w(inp["w_fourier"][l])
    w["wout"] = _tile_w(inp["w_out"][l])
    w["wfg"] = _tile_w(inp["w_ffn_gate"][l])
    w["wfu"] = _tile_w(inp["w_ffn_up"][l])
    w["wfd"] = _tile_w(inp["w_ffn_down"][l])
    return w


ACOLS = 6 * D // 8


def build_ada():
    nc = bass.Bass("TRN2", target_bir_lowering=False)
    din = lambda n, s, d: nc.dram_tensor(n, s, d, kind="ExternalInput").ap()
    cT = din("cT", [128, DC, 3], F32)
    wa = din("wa", [2, DC, 128, ACOLS], F32)
    ba = din("ba", [2, 3, ACOLS], F32)
    mo = nc.dram_tensor("mo", [2, 3, ACOLS], F32, kind="ExternalOutput").ap()
    with contextlib.ExitStack() as glob:
        gsb = lambda n, s, d: glob.enter_context(nc.sbuf_tensor(n, s, d))
        P = Prog(nc, glob)
        pb = [glob.enter_context(nc.psum_tensor("pb%d" % i, [128, 512], F32)) for i in range(6)]
        cs = gsb("cs", [128, DC, 3], F32)
        bs = gsb("bs", [3, 2, ACOLS], F32)
        os_ = gsb("os", [3, 2, ACOLS], F32)
        wbuf = [gsb("wbuf%d" % i, [128, 4, ACOLS], F32) for i in range(3)]
        P.dma("sync", cs[:], cT, writes=["cs"])
        P.dma("sync", bs[:], ba.rearrange("l r n -> r l n"), writes=["bs"])
        P.op("scalar", lambda E: E.activation(out=cs[:], in_=cs[:], func=AF.Silu), ["cs"], ["cs"])
        nb_ = 0
        for l in range(2):
            for q in range(4):
                wt = wbuf[nb_ % 3]
                wn = "wbuf%d" % (nb_ % 3)
                nb_ += 1
                P.dma("sync", wt[:], wa[l, 4 * q:4 * q + 4].rearrange("c p n -> p c n"), writes=[wn])
                for c in range(4):
                    for j in range(3):
                        bk = 3 * l + j
                        P.op("tensor", lambda E, c=c, j=j, bk=bk, wt=wt, q=q: E.matmul(pb[bk][0:3, :], lhsT=cs[:, 4 * q + c, :], rhs=wt[:, c, 512 * j:512 * j + 512],
                                                                                   start=(q == 0 and c == 0), stop=(q == 3 and c == 3)), ["cs", wn], ["pb%d" % bk])
            for j in range(3):
                bk = 3 * l + j
                P.op("vector", lambda E, l=l, j=j, bk=bk: E.tensor_tensor(out=os_[:, l, 512 * j:512 * j + 512], in0=pb[bk][0:3, :], in1=bs[:, l, 512 * j:512 * j + 512], op=ALU.add),
                     ["pb%d" % bk, "bs"], ["os"])
        P.dma("sync", mo.rearrange("l r n -> r l n"), os_[:], reads=["os"], key="mo")
        P.flush(barrier=False)
    return nc


_CACHE = {}


def _get(name, fn):
    if name not in _CACHE:
        _CACHE[name] = fn()
    return _CACHE[name]


def _run(nc, in_maps):
    import os, time, sys
    t0 = time.time()
    res = run_bass_kernel_spmd(nc, in_maps, core_ids=list(range(len(in_maps))))
    if os.environ.get("KDEBUG"):
        nb = sum(v.nbytes for v in in_maps[0].values())
        print("[kdebug] launch took %.1fs (in bytes/core %.1f MB)" % (time.time() - t0, nb / 1e6), file=sys.stderr, flush=True)
    return res.results


def _fm3(a):
    return np.ascontiguousarray(a.T.reshape(a.shape[1] // 128, 128, a.shape[0]))


def kernel(**inputs):
    inp = {k: np.asarray(v) for k, v in inputs.items()}
    B, L, CL, NCORE = 2, 4096, 256, 8
    LT = CL + L
    cstack = np.stack([inp["c"][0], inp["c"][1], inp["c_ctx"]], 0).astype(np.float32)
    cT = np.ascontiguousarray(cstack.reshape(3, DC, 128).transpose(2, 1, 0))
    maps = []
    for i in range(NCORE):
        cols = slice(i * ACOLS, (i + 1) * ACOLS)
        wa = np.ascontiguousarray(inp["w_ada"][:, :, cols]).reshape(2, DC, 128, ACOLS)
        ba = np.ascontiguousarray(np.broadcast_to(inp["b_ada"][:, None, cols], (2, 3, ACOLS)))
        maps.append(dict(cT=cT, wa=wa, ba=ba))
    r = _run(_get("ada", build_ada), maps)
    mod = np.concatenate([r[i]["mo"] for i in range(NCORE)], 2)

    XT = [np.concatenate([_fm3(inp["ctx"][b]), _fm3(inp["x"][b])], 2) for b in range(B)]
    consts = [mixer_consts(L, CL, POOL_WINDOWS[g], with_dft=False) for g in range(4)]
    dftL = dft_table(L)
    nc_mix = _get("mix", lambda: build_mixer(L, CL, cc=False))
    qidx = [np.concatenate([q * (CL // 4) + np.arange(CL // 4), CL + q * (L // 4) + np.arange(L // 4)]) for q in range(4)]
    for l in range(2):
        last = (l == 1)
        sh_m, sc_m, g_m, sh_f, sc_f, g_f = [mod[l][:, k * D:(k + 1) * D] for k in range(6)]
        maps = []
        for b in range(B):
            for g in range(4):
                m = mixer_in_map(None, inp["w_in"][l], inp["norm_mix_w"][l], sc_m[b], sh_m[b], sc_m[2], sh_m[2],
                                 inp["conv_w"][l], inp["conv_b"][l], inp["a_log"][l], inp["dt_bias"][l], inp["d_skip"][l],
                                 inp["ssd_norm_w"][l], g, consts[g])
                m["xT"] = XT[b]
                m["dftL"] = dftL
                maps.append(m)
        r = _run(nc_mix, maps)
        YT, DT, FT = [], [], []
        for b in range(B):
            yn = np.concatenate([np.asarray(r[4 * b + g]["yn"]) for g in range(4)], 1)
            YT.append(_fm3(yn))
            DT.append(np.concatenate([np.asarray(r[4 * b + g]["dT"]) for g in range(4)], 0))
            FT.append(np.concatenate([np.asarray(r[4 * b + g]["fT"]) for g in range(4)], 0))
        del r
        CB = 0 if last else 32
        LB, nblk = 512, 2
        use_cc = last
        nc_d = _get("dense%d" % l, lambda: build_dense(nblk, LB, CB, last, cc=use_cc))
        W = dense_weights(inp, l)
        misc = np.ascontiguousarray(np.stack([inp["pool_scale"][l], inp["final_norm_w"]]).astype(np.float32).reshape(2, DC, 128).transpose(2, 0, 1))
        maps = []
        tok_idx = []
        for i in range(NCORE):
            b, q = i // 4, i % 4
            idx = []
            for k in range(nblk):
                idx.append(CL + q * 1024 + k * LB + np.arange(LB))
                if CB:
                    idx.append(q * 64 + k * CB + np.arange(CB))
            idx = np.concatenate(idx)
            tok_idx.append(idx)
            md = np.zeros((2, 8, D), np.float32)
            for s_, row in ((0, b), (1, 2)):
                md[s_] = np.stack([inp["norm_mix_w"][l], sc_m[row], sh_m[row], g_m[row], inp["norm_ffn_w"][l], sc_f[row], sh_f[row], g_f[row]])
            m = dict(xin=np.ascontiguousarray(XT[b][:, :, idx]), yT=np.ascontiguousarray(YT[b][:, :, idx]),
                     dT=np.ascontiguousarray(DT[b][:, :, idx]), fT=np.ascontiguousarray(FT[b][:, :, idx]),
                     mods=np.ascontiguousarray(md.reshape(2, 8, DC, 128).transpose(3, 0, 1, 2)), misc=misc)
            m.update(shard_weights(W, i) if use_cc else W)
            maps.append(m)
        r = _run(nc_d, maps)
        for i in range(NCORE):
            XT[i // 4][:, :, tok_idx[i]] = r[i]["xo"]
        del r, maps, W
    out = np.stack([XT[b][:, :, CL:].reshape(D, L).T for b in range(B)], 0)
    return np.ascontiguousarray(out, dtype=np.float32)
```
